# Optimizing a Trainium2 kernel written in Bass

```python
import jax
import jax.numpy as jnp
from jax import lax
import numpy as np

D_MODEL = 1024
BATCH = 4
SEQ = 8192
DEPTH = 2

N_A_LAYERS = DEPTH // 2
N_B_LAYERS = DEPTH - N_A_LAYERS
HEAD_DIM = 64
N_HEADS_A = D_MODEL // HEAD_DIM
WIDTH_A = N_HEADS_A * HEAD_DIM
DECAY_LORA = 64
ICLR_LORA = 64
N_HEADS_B = D_MODEL // HEAD_DIM
WIDTH_B = N_HEADS_B * HEAD_DIM
MOBA_BLOCK = 256
MOBA_TOPK = 3
QUERY_CHUNK = 16
ROPE_THETA = 500000.0
ROPE_DIM = HEAD_DIM // 4
RMS_EPS = 1e-6
GN_EPS = 64e-5
NEG = -1e30

kernel_name = "yoco_rwkv7_moba_hybrid"


def rms_norm(x, g):
    xf = x.astype(jnp.float32)
    y = xf * lax.rsqrt(jnp.mean(xf * xf, axis=-1, keepdims=True) + RMS_EPS)
    return (y * g.astype(jnp.float32)).astype(x.dtype)


def rope_tables(T):
    inv_freq = jnp.power(jnp.float32(ROPE_THETA), -jnp.arange(0, ROPE_DIM, 2, dtype=jnp.float32) / ROPE_DIM)
    ang = jnp.arange(T, dtype=jnp.float32)[:, None] * inv_freq[None, :]
    return jnp.cos(ang), jnp.sin(ang)


def partial_rope(x, cos, sin):
    half = ROPE_DIM // 2
    xr = x[..., :ROPE_DIM].astype(jnp.float32)
    x1, x2 = xr[..., :half], xr[..., half:]
    rot = jnp.concatenate([x1 * cos - x2 * sin, x2 * cos + x1 * sin], axis=-1)
    return jnp.concatenate([rot.astype(x.dtype), x[..., ROPE_DIM:]], axis=-1)


def rwkv7_time_mix(h, mu, w_in, w0, w2, a0, a2, k_k, k_a, r_k, gn_w, gn_b, w_out):
    B, T, D = h.shape
    H, N = N_HEADS_A, HEAD_DIM
    xx = jnp.pad(h, ((0, 0), (1, 0), (0, 0)))[:, :-1] - h
    widths = (WIDTH_A, DECAY_LORA, WIDTH_A, WIDTH_A, ICLR_LORA, WIDTH_A)
    outs = []
    off = 0
    for i, wd in enumerate(widths):
        outs.append((h + xx * mu[i]) @ w_in[:, off:off + wd])
        off += wd
    r, wdn, k, v, adn, gp = [o.astype(jnp.float32) for o in outs]
    w = -jax.nn.softplus(-(w0 + jnp.tanh(wdn) @ w2)) - 0.5
    decay = jnp.exp(-jnp.exp(w))
    a = jax.nn.sigmoid(a0 + adn @ a2)
    kk = (k * k_k).reshape(B, T, H, N)
    kk = kk / jnp.maximum(jnp.linalg.norm(kk, axis=-1, keepdims=True), 1e-12)
    k = k * (1.0 + (a - 1.0) * k_a)

    def heads_tm(z):
        return z.reshape(B, T, H, N).transpose(1, 0, 2, 3)

    r4, d4, k4, v4 = heads_tm(r), heads_tm(decay), heads_tm(k), heads_tm(v)
    kk4 = heads_tm(kk)
    b4 = kk4 * heads_tm(a)

    def step(S, inp):
        r_t, d_t, k_t, v_t, kk_t, b_t = inp
        sa = jnp.einsum('bhvk,bhk->bhv', S, kk_t)
        S = S * d_t[:, :, None, :] - sa[..., None] * b_t[:, :, None, :] + v_t[..., None] * k_t[:, :, None, :]
        y = jnp.einsum('bhvk,bhk->bhv', S, r_t)
        return S, y

    S0 = jnp.zeros((B, H, N, N), jnp.float32)
    _, y = lax.scan(step, S0, (r4, d4, k4, v4, kk4, b4))
    y = y.transpose(1, 0, 2, 3)
    mean = jnp.mean(y, axis=-1, keepdims=True)
    var = jnp.mean(jnp.square(y - mean), axis=-1, keepdims=True)
    y = ((y - mean) * lax.rsqrt(var + GN_EPS)).reshape(B, T, D) * gn_w + gn_b
    bonus = jnp.sum(r.reshape(B, T, H, N) * k.reshape(B, T, H, N) * r_k, axis=-1, keepdims=True) * v.reshape(B, T, H, N)
    y = (y + bonus.reshape(B, T, D)) * jax.nn.silu(gp)
    return y.astype(h.dtype) @ w_out


def shared_kv(x, norm_kv, w_kv, k_norm, cos, sin):
    B, T, D = x.shape
    kv = rms_norm(x, norm_kv) @ w_kv
    k = kv[..., :WIDTH_B].reshape(B, T, N_HEADS_B, HEAD_DIM).transpose(0, 2, 1, 3)
    v = kv[..., WIDTH_B:].reshape(B, T, N_HEADS_B, HEAD_DIM).transpose(0, 2, 1, 3)
    k = partial_rope(rms_norm(k, k_norm), cos, sin)
    nb = -(-T // MOBA_BLOCK)
    pad = nb * MOBA_BLOCK - T
    k = jnp.pad(k, ((0, 0), (0, 0), (0, pad), (0, 0)))
    v = jnp.pad(v, ((0, 0), (0, 0), (0, pad), (0, 0)))
    kb = k.reshape(B, N_HEADS_B, nb, MOBA_BLOCK, HEAD_DIM)
    vb = v.reshape(B, N_HEADS_B, nb, MOBA_BLOCK, HEAD_DIM)
    kmean = jnp.mean(kb.astype(jnp.float32), axis=3)
    return kb, vb, kmean


def moba_attention(h, w_in, q_norm, w_out, kb, vb, kmean, cos, sin):
    B, T, D = h.shape
    H, hd, C, BS = N_HEADS_B, HEAD_DIM, QUERY_CHUNK, MOBA_BLOCK
    nb = kb.shape[2]
    topk = min(MOBA_TOPK, nb)
    scale = 1.0 / float(np.sqrt(hd))
    proj = h @ w_in
    q = proj[..., :WIDTH_B].reshape(B, T, H, hd).transpose(0, 2, 1, 3)
    gate = proj[..., WIDTH_B:]
    q = partial_rope(rms_norm(q, q_norm), cos, sin)
    nc = T // C
    qch = q.reshape(B, H, nc, C, hd).transpose(2, 0, 1, 3, 4)
    bi = jnp.arange(B)[:, None, None, None]
    hi = jnp.arange(H)[None, :, None, None]

    def chunk_fn(args):
        qc, ci = args
        cb = (ci * C) // BS
        qpos = ci * C + jnp.arange(C)
        gs = jnp.einsum('bhcd,bhnd->bhcn', qc.astype(jnp.float32), kmean)
        gs = jnp.where(jnp.arange(nb) < cb, gs, NEG)
        _, idx = lax.top_k(gs, topk)
        valid = idx < cb
        ksel = kb[bi, hi, idx]
        vsel = vb[bi, hi, idx]
        s_sel = jnp.einsum('bhcd,bhcjsd->bhcjs', qc, ksel).astype(jnp.float32) * scale
        s_sel = jnp.where(valid[..., None], s_sel, NEG).reshape(B, H, C, topk * BS)
        kown = lax.dynamic_index_in_dim(kb, cb, axis=2, keepdims=False)
        vown = lax.dynamic_index_in_dim(vb, cb, axis=2, keepdims=False)
        s_own = jnp.einsum('bhcd,bhsd->bhcs', qc, kown).astype(jnp.float32) * scale
        kpos = cb * BS + jnp.arange(BS)
        s_own = jnp.where(kpos[None, :] <= qpos[:, None], s_own, NEG)
        p = jax.nn.softmax(jnp.concatenate([s_sel, s_own], axis=-1), axis=-1)
        p_sel = p[..., :topk * BS].reshape(B, H, C, topk, BS).astype(vb.dtype)
        p_own = p[..., topk * BS:].astype(vb.dtype)
        return jnp.einsum('bhcjs,bhcjsd->bhcd', p_sel, vsel) + jnp.einsum('bhcs,bhsd->bhcd', p_own, vown)

    o = lax.map(chunk_fn, (qch, jnp.arange(nc)))
    o = o.transpose(1, 0, 3, 2, 4).reshape(B, T, WIDTH_B)
    return (o * jax.nn.silu(gate)) @ w_out


def setup_inputs(seed: int = 0) -> dict:
    key = jax.random.key(seed)
    ks = jax.random.split(key, 24)
    D, NA, NB_ = D_MODEL, N_A_LAYERS, N_B_LAYERS
    in_a = 4 * WIDTH_A + DECAY_LORA + ICLR_LORA
    nrm = jax.random.normal
    return {
        "x": nrm(ks[0], (BATCH, SEQ, D), jnp.float32),
        "norm_a": 1.0 + 0.02 * nrm(ks[1], (NA, D), jnp.float32),
        "mu_a": jax.random.uniform(ks[2], (NA, 6, D), jnp.float32),
        "w_in_a": nrm(ks[3], (NA, D, in_a), jnp.float32) * D ** -0.5,
        "w0_a": jax.random.uniform(ks[4], (NA, D), jnp.float32, -5.0, 0.0),
        "w2_a": nrm(ks[5], (NA, DECAY_LORA, D), jnp.float32) * 0.5 * DECAY_LORA ** -0.5,
        "a0_a": 0.1 * nrm(ks[6], (NA, D), jnp.float32),
        "a2_a": nrm(ks[7], (NA, ICLR_LORA, D), jnp.float32) * 0.5 * ICLR_LORA ** -0.5,
        "kk_a": 0.85 + 0.05 * nrm(ks[8], (NA, D), jnp.float32),
        "ka_a": 1.0 + 0.05 * nrm(ks[9], (NA, D), jnp.float32),
        "rk_a": 0.1 * nrm(ks[10], (NA, N_HEADS_A, HEAD_DIM), jnp.float32),
        "gn_w_a": 1.0 + 0.02 * nrm(ks[11], (NA, D), jnp.float32),
        "gn_b_a": 0.02 * nrm(ks[12], (NA, D), jnp.float32),
        "w_out_a": nrm(ks[13], (NA, WIDTH_A, D), jnp.float32) * WIDTH_A ** -0.5,
        "norm_kv": 1.0 + 0.02 * nrm(ks[14], (D,), jnp.float32),
        "w_kv": nrm(ks[15], (D, 2 * WIDTH_B), jnp.float32) * D ** -0.5,
        "k_norm": 1.0 + 0.02 * nrm(ks[16], (HEAD_DIM,), jnp.float32),
        "norm_b": 1.0 + 0.02 * nrm(ks[17], (NB_, D), jnp.float32),
        "w_in_b": nrm(ks[18], (NB_, D, 2 * WIDTH_B), jnp.float32) * D ** -0.5,
        "q_norm_b": 1.0 + 0.02 * nrm(ks[19], (NB_, HEAD_DIM), jnp.float32),
        "w_out_b": nrm(ks[20], (NB_, WIDTH_B, D), jnp.float32) * WIDTH_B ** -0.5,
    }


def reference(x, norm_a, mu_a, w_in_a, w0_a, w2_a, a0_a, a2_a, kk_a, ka_a, rk_a, gn_w_a, gn_b_a, w_out_a,
              norm_kv, w_kv, k_norm, norm_b, w_in_b, q_norm_b, w_out_b):
    T = x.shape[1]
    cos, sin = rope_tables(T)
    kb = vb = kmean = None
    for layer in range(DEPTH):
        if layer < N_A_LAYERS:
            i = layer
            x = x + rwkv7_time_mix(rms_norm(x, norm_a[i]), mu_a[i], w_in_a[i], w0_a[i], w2_a[i], a0_a[i], a2_a[i],
                                   kk_a[i], ka_a[i], rk_a[i], gn_w_a[i], gn_b_a[i], w_out_a[i])
        else:
            if layer == N_A_LAYERS:
                kb, vb, kmean = shared_kv(x, norm_kv, w_kv, k_norm, cos, sin)
            j = layer - N_A_LAYERS
            x = x + moba_attention(rms_norm(x, norm_b[j]), w_in_b[j], q_norm_b[j], w_out_b[j], kb, vb, kmean, cos, sin)
    return x
```

```python
from contextlib import ExitStack
import numpy as np
import concourse.bass as bass
import concourse.mybir as mybir
from concourse.bass_utils import run_bass_kernel_spmd

F32 = mybir.dt.float32
BF16 = mybir.dt.bfloat16
AF = mybir.ActivationFunctionType
ALU = mybir.AluOpType
AX = mybir.AxisListType

D_MODEL = 1024
HD = 64
HL = 8
FL = HL * HD
CH = 64
RMS_EPS = 1e-6
GN_EPS = 64e-5
C0 = float(np.exp(-0.5))


class Buf:
    __slots__ = ("name", "lw", "rd", "excl")

    def __init__(self, name=""):
        self.name = name
        self.lw = None
        self.rd = {}
        self.excl = False


class Tl:
    def __init__(self, t, name):
        self.t = t
        self.b = Buf(name)

    def __getitem__(self, idx):
        return self.t[idx]


class Sched:
    ENGS = ("tensor", "vector", "scalar", "gpsimd", "sync")

    def __init__(self, nc, es, n_dma_slots=12):
        self.nc = nc
        self.es = es
        self.prog = {e: [] for e in self.ENGS}
        self.sems = {}
        self.cnt = {}
        for e in self.ENGS:
            self.sems[e] = es.enter_context(nc.semaphore("s_" + e))
            self.cnt[e] = 0
        self.dma_slots = {}
        for q in ("sync", "gpsimd"):
            sl = []
            for i in range(n_dma_slots):
                k = "d_%s_%d" % (q, i)
                self.sems[k] = es.enter_context(nc.semaphore(k))
                self.cnt[k] = 0
                sl.append(k)
            self.dma_slots[q] = sl
        self.dma_rr = {q: 0 for q in self.dma_slots}
        self.waited = {}
        self.same_engine_sync = {"vector": True, "scalar": True, "gpsimd": True, "tensor": False, "sync": False}
        self.n_inst = 0

    def _wait(self, eng, key, val):
        if val <= 0 or self.waited.get((eng, key), 0) >= val:
            return
        self.waited[(eng, key)] = val
        self.prog[eng].append(("wait", self.sems[key], val))

    def _deps(self, eng, reads, writes, own_skip):
        deps = {}
        for b in reads:
            if b.lw is not None:
                k, v = b.lw
                deps[k] = max(deps.get(k, 0), v)
            if b.excl:
                for k, v in b.rd.items():
                    if k != eng:
                        deps[k] = max(deps.get(k, 0), v)
        for b in writes:
            if b.lw is not None:
                k, v = b.lw
                deps[k] = max(deps.get(k, 0), v)
            for k, v in b.rd.items():
                deps[k] = max(deps.get(k, 0), v)
        for k, v in deps.items():
            if k == eng and own_skip:
                continue
            self._wait(eng, k, v)

    def _mark(self, tok, reads, writes):
        k, v = tok
        for b in writes:
            b.lw = tok
            b.rd = {}
        for b in reads:
            b.rd[k] = max(b.rd.get(k, 0), v)

    def op(self, eng, fn, reads=(), writes=(), inc=True):
        reads = [x.b if isinstance(x, Tl) else x for x in reads]
        writes = [x.b if isinstance(x, Tl) else x for x in writes]
        self._deps(eng, reads, writes, not self.same_engine_sync[eng])
        if inc:
            self.cnt[eng] += 1
            tok = (eng, self.cnt[eng])
            self.prog[eng].append(("inst", fn, self.sems[eng], 1))
        else:
            tok = (eng, self.cnt[eng] + 1)
            self.prog[eng].append(("inst", fn, None, 0))
        self._mark(tok, reads, writes)
        self.n_inst += 1

    def dma(self, q, out, in_, reads=(), writes=(), **kw):
        reads = [x.b if isinstance(x, Tl) else x for x in reads]
        writes = [x.b if isinstance(x, Tl) else x for x in writes]
        sl = self.dma_slots[q]
        key = sl[self.dma_rr[q] % len(sl)]
        self.dma_rr[q] += 1
        self._wait(q, key, self.cnt[key])
        self._deps(q, reads, writes, True)
        self.cnt[key] += 16
        tok = (key, self.cnt[key])
        self.prog[q].append(("inst", lambda e, o=out, i=in_, kw=kw: e.dma_start(out=o, in_=i, **kw), self.sems[key], 16))
        self._mark(tok, reads, writes)
        self.n_inst += 1

    def barrier(self):
        for e in self.ENGS:
            for k in self.sems:
                if k != e or e == "tensor":
                    self._wait(e, k, self.cnt[k])

    def final_wait(self, eng, bufs):
        for b in bufs:
            if b.lw is not None:
                self._wait(eng, b.lw[0], b.lw[1])

    def emit(self):
        with self.nc.Block() as block:
            for e in self.ENGS:
                prog = self.prog[e]
                if not prog:
                    continue

                def body(eh, prog=prog):
                    for it in prog:
                        if it[0] == "wait":
                            eh.wait_ge(it[1], it[2])
                        else:
                            ins = it[1](eh)
                            if it[2] is not None:
                                ins.then_inc(it[2], it[3])
                getattr(block, e)(body)


class Ctx:
    def __init__(self, nc, es):
        self.nc = nc
        self.es = es
        self.S = Sched(nc, es)
        self.n = 0
        self.out_bufs = []

    def sb(self, shape, dt, name=None):
        self.n += 1
        name = "%s_%d" % (name or "t", self.n)
        return Tl(self.es.enter_context(self.nc.sbuf_tensor(name, list(shape), dt)), name)

    def ps(self, shape, dt, name=None):
        self.n += 1
        name = "%s_%d" % (name or "p", self.n)
        t = Tl(self.es.enter_context(self.nc.psum_tensor(name, list(shape), dt)), name)
        t.b.excl = True
        return t

    def tt(self, eng, o, a, b, op, wr, rd):
        self.S.op(eng, lambda e: e.tensor_tensor(out=o, in0=a, in1=b, op=op), reads=rd, writes=wr)

    def ts(self, eng, o, a, s1, op0, wr, rd, s2=None, op1=None):
        if op1 is None:
            self.S.op(eng, lambda e: e.tensor_scalar(out=o, in0=a, scalar1=s1, scalar2=None, op0=op0), reads=rd, writes=wr)
        else:
            self.S.op(eng, lambda e: e.tensor_scalar(out=o, in0=a, scalar1=s1, scalar2=s2, op0=op0, op1=op1), reads=rd, writes=wr)

    def stt(self, o, a, s, b, op0, op1, wr, rd):
        self.S.op("vector", lambda e: e.scalar_tensor_tensor(out=o, in0=a, scalar=s, in1=b, op0=op0, op1=op1), reads=rd, writes=wr)

    def act(self, o, a, func, wr, rd, scale=1.0, bias=None, accum=None):
        def f(e):
            kw = {}
            if bias is not None:
                kw["bias"] = bias
            if accum is not None:
                kw["accum_out"] = accum
            return e.activation(out=o, in_=a, func=func, scale=scale, **kw)
        self.S.op("scalar", f, reads=rd, writes=wr)

    def cp(self, eng, o, a, wr, rd):
        if eng == "scalar":
            self.act(o, a, AF.Copy, wr, rd)
        else:
            self.S.op(eng, lambda e: e.tensor_copy(out=o, in_=a), reads=rd, writes=wr)

    def red(self, o, a, wr, rd, op=ALU.add):
        self.S.op("vector", lambda e: e.tensor_reduce(out=o, in_=a, axis=AX.X, op=op), reads=rd, writes=wr)

    def mm(self, o, lhsT, rhs, wr, rd, start=True, stop=True, inc=True):
        self.S.op("tensor", lambda e: e.matmul(o, lhsT=lhsT, rhs=rhs, start=start, stop=stop), reads=rd, writes=wr, inc=inc)

    def tr(self, o, a, ident, wr, rd, inc=True):
        self.S.op("tensor", lambda e: e.transpose(out=o, in_=a, identity=ident), reads=rd, writes=wr, inc=inc)

    def dma(self, q, o, a, wr, rd):
        self.S.dma(q, o, a, reads=rd, writes=wr)


def make_consts(K):
    c = {}
    idf = K.sb([128, 128], F32, "idf")
    K.S.op("gpsimd", lambda e: e.memset(idf[:], 0.0), writes=[idf])
    K.S.op("gpsimd", lambda e: e.affine_select(out=idf[:], in_=idf[:], pattern=[[-1, 128]], compare_op=ALU.not_equal,
                                               fill=1.0, base=0, channel_multiplier=1), reads=[idf], writes=[idf])
    idb = K.sb([128, 128], BF16, "idb")
    K.cp("vector", idb[:], idf[:], [idb], [idf])
    c["idf"], c["idb"] = idf, idb
    return c


def tri_mask(K, name, n, keep_if, fill_other=0.0, val=1.0, reps=1):
    t = K.sb([n, reps, n], F32, name)
    K.S.op("gpsimd", lambda e: e.memset(t[:], val), writes=[t])
    cmp = ALU.is_gt if keep_if == "lt" else ALU.is_ge
    K.S.op("gpsimd", lambda e: e.affine_select(out=t[:], in_=t[:], pattern=[[0, reps], [1, n]], compare_op=cmp,
                                               fill=0.0, base=0, channel_multiplier=-1), reads=[t], writes=[t])
    return t


def phase_a1(K, T, d, cst):
    nc, S = K.nc, K.S
    NT = T // 128
    idb = cst["idb"]
    W = K.sb([128, 8, 2176], BF16, "W")
    Wm = K.sb([128, 8, 2176], BF16, "Wm")
    mu = K.sb([128, 8, 6], F32, "mu")
    K.dma("sync", mu[:], d["mu"], [mu], [])
    wst = [K.sb([128, 2176], F32, "wst") for _ in range(2)]
    goff = [(0, 512), (2048, 64), (512, 512), (1024, 512), (2112, 64), (1536, 512)]
    for c in range(8):
        st = wst[c % 2]
        K.dma("sync", st[:], d["wA"][c * 128:(c + 1) * 128, :], [st], [])
        K.cp("gpsimd", W[:, c, :], st[:], [W], [st])
        for gi, (o, n) in enumerate(goff):
            K.ts("vector", Wm[:, c, o:o + n], st[:, o:o + n], mu[:, c, gi:gi + 1], ALU.mult, [Wm], [st, mu])
    def rep(name, n=FL):
        t = K.sb([128, n], F32, name)
        K.dma("sync", t[:], d[name].partition_broadcast(128), [t], [])
        return t
    normrep = rep("norm_a", D_MODEL)
    w0r, a0r, kkr_, kar, rkr = rep("w0"), rep("a0"), rep("kk"), rep("ka"), rep("rk")
    omk = K.sb([128, FL], F32, "omk")
    K.ts("vector", omk[:], kar[:], -1.0, ALU.mult, [omk], [kar], 1.0, ALU.add)
    w2 = K.sb([128, FL], F32, "w2")
    a2 = K.sb([128, FL], F32, "a2")
    S.op("gpsimd", lambda e: e.memset(w2[:], 0.0), writes=[w2])
    S.op("gpsimd", lambda e: e.memset(a2[:], 0.0), writes=[a2])
    K.dma("sync", w2[0:64, :], d["w2"], [w2], [])
    K.dma("sync", a2[64:128, :], d["a2"], [a2], [])
    tri = K.sb([128, 128], F32, "tri")
    S.op("gpsimd", lambda e: e.memset(tri[:], 1.0), writes=[tri])
    S.op("gpsimd", lambda e: e.affine_select(out=tri[:], in_=tri[:], pattern=[[1, 128]], compare_op=ALU.is_ge, fill=0.0,
                                             base=0, channel_multiplier=-1), reads=[tri], writes=[tri])
    S.op("gpsimd", lambda e: e.memset(tri[0:64, 64:128], 0.0), reads=[tri], writes=[tri])
    ind = K.sb([128, 2], F32, "ind")
    S.op("gpsimd", lambda e: e.memset(ind[:], 0.0), writes=[ind])
    S.op("gpsimd", lambda e: e.memset(ind[0:64, 0:1], 1.0), reads=[ind], writes=[ind])
    S.op("gpsimd", lambda e: e.memset(ind[64:128, 1:2], 1.0), reads=[ind], writes=[ind])
    epsb = K.sb([128, 1], F32, "epsb")
    S.op("gpsimd", lambda e: e.memset(epsb[:], RMS_EPS), writes=[epsb])

    xt = [K.sb([128, D_MODEL], F32, "xt") for _ in range(2)]
    hT = [K.sb([128, 8, 128], BF16, "hT") for _ in range(2)]
    prevc = K.sb([128, 8, 1], BF16, "prevc")
    S.op("gpsimd", lambda e: e.memset(prevc[:], 0.0), writes=[prevc])
    junk = K.sb([128, D_MODEL], BF16, "junk")
    ss = K.sb([128, 1], F32, "ss")
    rt = K.sb([128, 1], F32, "rt")
    rstd = K.sb([128, 1], F32, "rstd")
    hb = K.sb([128, D_MODEL], BF16, "hb")
    xxT = K.sb([128, 8, 128], BF16, "xxT")
    LT = K.sb([128, 128], F32, "LT")
    f = lambda n: K.sb([128, FL], F32, n)
    z, sg, az, a_, cq, eP, eN, eQ = f("z"), f("sg"), f("az"), f("a"), f("cq"), f("eP"), f("eN"), f("eQ")
    kkraw, sq, kk, t1, t2, kt, bb, rk1, rk2 = f("kkraw"), f("sq"), f("kk"), f("t1"), f("t2"), f("kt"), f("bb"), f("rk1"), f("rk2")
    ssh = K.sb([128, HL], F32, "ssh")
    rn = K.sb([128, HL], F32, "rn")
    rks = K.sb([128, HL], F32, "rks")
    pcs = K.sb([128, 4, 2 * NT], F32, "pcs")
    g = lambda n: [K.sb([128, FL], BF16, n) for _ in range(2)]
    al, be, nbe, ka_, rho, vb = g("al"), g("be"), g("nbe"), g("ka"), g("rho"), g("vb")
    gT = lambda n: [K.sb([128, 4, 128], BF16, n) for _ in range(2)]
    alT, beT, rhoT, kaT = gT("alT"), gT("beT"), gT("rhoT"), gT("kaT")
    bonus = [K.sb([128, FL], F32, "bonus") for _ in range(2)]
    sgt = [K.sb([128, FL], F32, "sgt") for _ in range(2)]
    pr, pk, pv, pg, pw, pa, pl = [K.ps([128, 512], F32, n) for n in ("pr", "pk", "pv", "pg", "pw", "pa", "pl")]
    pT = K.ps([128, 1024], BF16, "pT")
    v3 = lambda ap: ap.rearrange("p (h n) -> p h n", h=HL)
    bc = lambda ap: ap.unsqueeze(2).to_broadcast([128, HL, HD])

    import os
    LV = int(os.environ.get("A1STOP", "9"))
    if LV == 0:
        return
    for j in range(NT):
        jb = j % 2
        X, H, Hn = xt[jb], hT[jb], hT[1 - jb]
        r0 = j * 128
        K.dma("sync", X[:], d["x"][r0:r0 + 128, :], [X], [])
        K.act(junk[:], X[:], AF.Square, [junk, ss], [X], accum=ss[:])
        K.act(rt[:], ss[:], AF.Sqrt, [rt], [ss, epsb], scale=1.0 / D_MODEL, bias=epsb[:])
        S.op("vector", lambda e: e.reciprocal(out=rstd[:], in_=rt[:]), reads=[rt], writes=[rstd])
        K.stt(hb[:], X[:], rstd[:, 0:1], normrep[:], ALU.mult, ALU.mult, [hb], [X, rstd, normrep])
        for c in range(8):
            K.tr(pT[:, c * 128:(c + 1) * 128], hb[:, c * 128:(c + 1) * 128], idb[:], [pT], [hb, idb], inc=(c == 7))
        K.cp("scalar", H[:], pT[:].rearrange("p (c t) -> p c t", c=8), [H], [pT])
        K.tt("gpsimd", xxT[:, :, 1:128], H[:, :, 0:127], H[:, :, 1:128], ALU.subtract, [xxT], [H])
        K.tt("gpsimd", xxT[:, :, 0:1], prevc[:], H[:, :, 0:1], ALU.subtract, [xxT], [H, prevc])
        if j + 1 < NT:
            K.cp("gpsimd", prevc[:], H[:, :, 127:128], [prevc], [H])
        if LV == 1:
            continue
        NOM = os.environ.get("A1NOM")
        NOL = os.environ.get("A1NOL")
        for gi, (pp, o) in enumerate(((pr, 0), (pk, 512), (pv, 1024), (pg, 1536))):
            for c in range(8):
                K.mm(pp[:, :], H[:, c, :], W[:, c, o:o + 512], [pp], [H, W], start=(c == 0), stop=bool(NOM and c == 7), inc=bool(NOM and c == 7))
            if NOM:
                continue
            for c in range(8):
                K.mm(pp[:, :], xxT[:, c, :], Wm[:, c, o:o + 512], [pp], [xxT, Wm], start=False, stop=(c == 7), inc=(c == 7))
        for c in range(8):
            K.mm(pl[:, 0:128], W[:, c, 2048:2176], H[:, c, :], [pl], [H, W], start=(c == 0), stop=False, inc=False)
        for c in range(8):
            K.mm(pl[:, 0:128], Wm[:, c, 2048:2176], xxT[:, c, :], [pl], [xxT, Wm], start=False, stop=(c == 7), inc=(c == 7))
        if LV == 2:
            continue
        K.act(LT[0:64, :], pl[0:64, 0:128], AF.Tanh, [LT], [pl])
        K.cp("vector", LT[64:128, :], pl[64:128, 0:128], [LT], [pl])
        K.mm(pw[:, :], LT[:], w2[:], [pw], [LT, w2])
        K.mm(pa[:, :], LT[:], a2[:], [pa], [LT, a2])
        K.tt("vector", z[:], pw[:, :], w0r[:], ALU.add, [z], [pw, w0r])
        K.act(sg[:], z[:], AF.Sigmoid, [sg], [z])
        K.tt("vector", az[:], pa[:, :], a0r[:], ALU.add, [az], [pa, a0r])
        K.act(a_[:], az[:], AF.Sigmoid, [a_], [az])
        K.mm(pw[:, :], tri[:], sg[:], [pw], [tri, sg])
        for fc in range(4):
            K.mm(pl[:, 256 + fc * 2:256 + fc * 2 + 2], sg[:, fc * 128:(fc + 1) * 128], ind[:], [pl], [sg, ind], inc=(fc == 3))
        K.act(eP[:], pw[:, :], AF.Exp, [eP], [pw], scale=-C0)
        K.act(eN[:], pw[:, :], AF.Exp, [eN], [pw], scale=C0)
        K.tt("vector", cq[:], pw[:, :], sg[:], ALU.subtract, [cq], [pw, sg])
        K.act(eQ[:], cq[:], AF.Exp, [eQ], [cq], scale=-C0)
        K.act(pcs[:, :, 2 * j:2 * j + 2], pl[:, 256:264].rearrange("p (fc c) -> p fc c", fc=4), AF.Exp, [pcs], [pl], scale=-C0)
        if LV == 3:
            continue
        K.tt("vector", kkraw[:], pk[:, :], kkr_[:], ALU.mult, [kkraw], [pk, kkr_])
        K.tt("gpsimd", sq[:], kkraw[:], kkraw[:], ALU.mult, [sq], [kkraw])
        K.red(ssh[:], v3(sq[:]), [ssh], [sq])
        K.act(rn[:], ssh[:], AF.Sqrt, [rn], [ssh])
        K.ts("vector", rn[:], rn[:], 1e-12, ALU.max, [rn], [rn])
        S.op("vector", lambda e: e.reciprocal(out=rn[:], in_=rn[:]), reads=[rn], writes=[rn])
        K.tt("vector", v3(kk[:]), v3(kkraw[:]), bc(rn[:]), ALU.mult, [kk], [kkraw, rn])
        K.tt("gpsimd", t1[:], a_[:], kar[:], ALU.mult, [t1], [a_, kar])
        K.tt("gpsimd", t2[:], t1[:], omk[:], ALU.add, [t2], [t1, omk])
        K.tt("vector", kt[:], pk[:, :], t2[:], ALU.mult, [kt], [pk, t2])
        K.tt("gpsimd", bb[:], kk[:], a_[:], ALU.mult, [bb], [kk, a_])
        A, Bp, NB, KA, RH, VB = al[jb], be[jb], nbe[jb], ka_[jb], rho[jb], vb[jb]
        K.tt("gpsimd", A[:], kk[:], eQ[:], ALU.mult, [A], [kk, eQ])
        K.tt("gpsimd", Bp[:], bb[:], eN[:], ALU.mult, [Bp], [bb, eN])
        K.ts("gpsimd", NB[:], Bp[:], -1.0, ALU.mult, [NB], [Bp])
        K.tt("gpsimd", KA[:], kt[:], eN[:], ALU.mult, [KA], [kt, eN])
        K.tt("vector", RH[:], pr[:, :], eP[:], ALU.mult, [RH], [pr, eP])
        K.tt("vector", rk1[:], pr[:, :], rkr[:], ALU.mult, [rk1], [pr, rkr])
        K.tt("gpsimd", rk2[:], rk1[:], kt[:], ALU.mult, [rk2], [rk1, kt])
        K.red(rks[:], v3(rk2[:]), [rks], [rk2])
        BO, SG = bonus[jb], sgt[jb]
        K.tt("vector", v3(BO[:]), v3(pv[:, :]), bc(rks[:]), ALU.mult, [BO], [pv, rks])
        K.cp("scalar", VB[:], pv[:, :], [VB], [pv])
        K.act(SG[:], pg[:, :], AF.Silu, [SG], [pg])
        if LV == 4:
            continue
        for (q0, q1, o0, o1) in ((A, Bp, alT[jb], beT[jb]), (RH, KA, rhoT[jb], kaT[jb])):
            for qi, q in enumerate((q0, q1)):
                for fc in range(4):
                    K.tr(pT[:, qi * 512 + fc * 128:qi * 512 + (fc + 1) * 128], q[:, fc * 128:(fc + 1) * 128], idb[:], [pT], [q, idb],
                         inc=(qi == 1 and fc == 3))
            K.cp("scalar", o0[:], pT[:, 0:512].rearrange("p (c t) -> p c t", c=4), [o0], [pT])
            K.cp("vector", o1[:], pT[:, 512:1024].rearrange("p (c t) -> p c t", c=4), [o1], [pT])
        if LV == 5:
            continue
        for nm, src in (("al", A), ("nbe", NB), ("kap", KA), ("vb", VB), ("bonus", BO), ("sgt", SG)):
            K.dma("sync", d[nm][r0:r0 + 128, :], src[:], [], [src])
        for nm, src in (("alT", alT[jb]), ("beT", beT[jb]), ("rhoT", rhoT[jb]), ("kapT", kaT[jb])):
            K.dma("gpsimd", d[nm].rearrange("(fc p) t -> p fc t", p=128)[:, :, r0:r0 + 128], src[:], [], [src])
    K.dma("sync", d["PC"].rearrange("(fc p) c -> p fc c", p=128), pcs[:], [], [pcs])


def mask_tile(K, name, n, reps, cm, step, cmp, val):
    t = K.sb([n, reps, n], F32, name)
    K.S.op("gpsimd", lambda e: e.memset(t[:], val), writes=[t])
    K.S.op("gpsimd", lambda e: e.affine_select(out=t[:], in_=t[:], pattern=[[0, reps], [step, n]], compare_op=cmp,
                                               fill=0.0, base=0, channel_multiplier=cm), reads=[t], writes=[t])
    return t


def phase_scan(K, T, d):
    S = K.S
    NCH = T // CH
    SUP = 4
    NS = NCH // SUP
    m_lt = mask_tile(K, "m_lt", 64, HL, -1, 1, ALU.is_gt, 1.0)
    m_le = mask_tile(K, "m_le", 64, HL, -1, 1, ALU.is_ge, 1.0)
    m_len = mask_tile(K, "m_len", 64, HL, -1, 1, ALU.is_ge, -1.0)
    m_gt = mask_tile(K, "m_gt", 64, HL, 1, -1, ALU.is_gt, 1.0)
    idrep = K.sb([64, HL, 64], F32, "idrep")
    S.op("gpsimd", lambda e: e.memset(idrep[:], 0.0), writes=[idrep])
    S.op("gpsimd", lambda e: e.affine_select(out=idrep[:], in_=idrep[:], pattern=[[0, HL], [-1, 64]], compare_op=ALU.not_equal,
                                             fill=1.0, base=0, channel_multiplier=1), reads=[idrep], writes=[idrep])
    PCt = K.sb([64, HL, NCH], F32, "PCt")
    K.dma("sync", PCt[:], d["PC"].rearrange("(h k) c -> k h c", k=64), [PCt], [])
    Mf = K.sb([64, FL], F32, "Mf")
    Mb = [K.sb([64, FL], BF16, "Mb") for _ in range(2)]
    S.op("gpsimd", lambda e: e.memset(Mf[:], 0.0), writes=[Mf])
    S.op("gpsimd", lambda e: e.memset(Mb[0][:], 0.0), writes=[Mb[0]])
    fm = {n: [K.sb([64, HL, SUP * 64], BF16, n) for _ in range(2)] for n in ("alT", "beT", "rhoT", "kapT")}
    tm = {n: [K.sb([64, SUP, FL], BF16, n) for _ in range(2)] for n in ("al", "nbe", "kap", "vb")}
    w2 = lambda n, dt=BF16: [K.sb([64, FL], dt, n) for _ in range(2)]
    Nn, NTt, ARBn, AKT, ARK, WT, Z, Ub = w2("Nn"), w2("NTt"), w2("ARBn"), w2("AKT"), w2("ARK"), w2("WT"), w2("Z"), w2("Ub")
    Rr = [K.sb([64, FL], BF16, "R") for _ in range(4)]
    Xx = [K.sb([64, FL], BF16, "X") for _ in range(4)]
    XTt = [K.sb([64, FL], BF16, "XT") for _ in range(4)]
    Ysb = w2("Ysb", F32)
    tmp = K.sb([64, FL], F32, "tmp")
    banks = [K.ps([128, 512], F32, "bk") for _ in range(8)]
    bctr = [0]

    def bank():
        b = banks[bctr[0] % 8]
        bctr[0] += 1
        return b
    hs = lambda h: slice(h * 64, (h + 1) * 64)
    m2 = lambda t: t[:].rearrange("p h n -> p (h n)")
    Rfin = {}

    def load(st):
        sb_ = st % 2
        t0 = st * SUP * 64
        for n in fm:
            K.dma("sync", fm[n][sb_][:], d[n].rearrange("(h k) t -> k h t", k=64)[:, :, t0:t0 + SUP * 64], [fm[n][sb_]], [])
        for n in tm:
            K.dma("sync", tm[n][sb_][:], d[n][t0:t0 + SUP * 64, :].rearrange("(c s) f -> s c f", s=64), [tm[n][sb_]], [])

    def pre(c):
        st, cc, cb = c // SUP, c % SUP, c % 2
        sb_ = st % 2
        aT, bT, rT, kT = (fm[n][sb_] for n in ("alT", "beT", "rhoT", "kapT"))
        al_, vb_ = tm["al"][sb_], tm["vb"][sb_]
        ts_ = slice(cc * 64, (cc + 1) * 64)
        specs = ((bT, aT, m_lt, Nn[cb]), (aT, bT, m_gt, NTt[cb]), (bT, rT, m_len, ARBn[cb]), (kT, aT, m_lt, AKT[cb]), (kT, rT, m_le, ARK[cb]))
        for (L, R_, msk, dst) in specs:
            pb = bank()
            for h in range(HL):
                K.mm(pb[0:64, hs(h)], L[:, h, ts_], R_[:, h, ts_], [pb], [L, R_], inc=(h == HL - 1))
            K.tt("vector", dst[:], pb[0:64, :], m2(msk), ALU.mult, [dst], [pb, msk])
        r = Rr[(2 * cb) % 4]
        K.tt("gpsimd", r[:], m2(idrep), Nn[cb][:], ALU.subtract, [r], [idrep, Nn[cb]])
        X, XT = Nn[cb], NTt[cb]
        ri = 0
        for lvl in range(1, 6):
            last = lvl == 5
            Xn, XTn = Xx[2 * cb + lvl % 2], XTt[2 * cb + lvl % 2]
            if not last:
                pb = bank()
                for h in range(HL):
                    K.mm(pb[0:64, hs(h)], XT[:, hs(h)], X[:, hs(h)], [pb], [XT, X], inc=(h == HL - 1))
                K.cp("scalar", Xn[:], pb[0:64, :], [Xn], [pb])
            pb2 = bank()
            for h in range(HL):
                K.mm(pb2[0:64, hs(h)], X[:, hs(h)], XT[:, hs(h)], [pb2], [XT, X], inc=(h == HL - 1))
            K.cp("scalar", XTn[:], pb2[0:64, :], [XTn], [pb2])
            pb3 = bank()
            for h in range(HL):
                K.mm(pb3[0:64, hs(h)], XTn[:, hs(h)], r[:, hs(h)], [pb3], [XTn, r], inc=(h == HL - 1))
            ri += 1
            rn_ = Rr[(2 * cb + (ri % 2)) % 4]
            K.tt("vector", rn_[:], pb3[0:64, :], r[:], ALU.add, [rn_], [pb3, r])
            r = rn_
            X, XT = Xn, XTn
        Rfin[c] = r
        pb = bank()
        for h in range(HL):
            K.mm(pb[0:64, hs(h)], al_[:, cc, hs(h)], r[:, hs(h)], [pb], [al_, r], inc=(h == HL - 1))
        K.cp("scalar", WT[cb][:], pb[0:64, :], [WT[cb]], [pb])
        pb = bank()
        for h in range(HL):
            K.mm(pb[0:64, hs(h)], AKT[cb][:, hs(h)], vb_[:, cc, hs(h)], [pb], [AKT[cb], vb_], inc=(h == HL - 1))
        K.cp("scalar", Z[cb][:], pb[0:64, :], [Z[cb]], [pb])

    def seq(c):
        st, cc, cb = c // SUP, c % SUP, c % 2
        sb_ = st % 2
        rT = fm["rhoT"][sb_]
        nb_, ka_, vb_ = tm["nbe"][sb_], tm["kap"][sb_], tm["vb"][sb_]
        ts_ = slice(cc * 64, (cc + 1) * 64)
        r = Rfin.pop(c)
        M0, M1 = Mb[cb], Mb[1 - cb]
        pU = bank()
        for h in range(HL):
            K.mm(pU[0:64, hs(h)], r[:, hs(h)], Z[cb][:, hs(h)], [pU], [r, Z[cb]], start=True, stop=False, inc=False)
            K.mm(pU[0:64, hs(h)], WT[cb][:, hs(h)], M0[:, hs(h)], [pU], [WT[cb], M0], start=False, stop=True, inc=(h == HL - 1))
        K.cp("scalar", Ub[cb][:], pU[0:64, :], [Ub[cb]], [pU])
        pD = bank()
        for h in range(HL):
            K.mm(pD[0:64, hs(h)], ka_[:, cc, hs(h)], vb_[:, cc, hs(h)], [pD], [ka_, vb_], start=True, stop=False, inc=False)
            K.mm(pD[0:64, hs(h)], nb_[:, cc, hs(h)], Ub[cb][:, hs(h)], [pD], [nb_, Ub[cb]], start=False, stop=True, inc=(h == HL - 1))
        pY = bank()
        for h in range(HL):
            K.mm(pY[0:64, hs(h)], rT[:, h, ts_], M0[:, hs(h)], [pY], [rT, M0], start=True, stop=False, inc=False)
            K.mm(pY[0:64, hs(h)], ARBn[cb][:, hs(h)], Ub[cb][:, hs(h)], [pY], [ARBn[cb], Ub[cb]], start=False, stop=False, inc=False)
            K.mm(pY[0:64, hs(h)], ARK[cb][:, hs(h)], vb_[:, cc, hs(h)], [pY], [ARK[cb], vb_], start=False, stop=True, inc=(h == HL - 1))
        K.tt("vector", tmp[:], pD[0:64, :], Mf[:], ALU.add, [tmp], [pD, Mf])
        K.tt("vector", Mf[:].rearrange("p (h n) -> p h n", h=HL), tmp[:].rearrange("p (h n) -> p h n", h=HL),
             PCt[:, :, c:c + 1].to_broadcast([64, HL, 64]), ALU.mult, [Mf], [tmp, PCt])
        K.cp("gpsimd", M1[:], Mf[:], [M1], [Mf])
        K.cp("scalar", Ysb[cb][:], pY[0:64, :], [Ysb[cb]], [pY])
        K.dma("sync", d["Y"][c * 64:(c + 1) * 64, :], Ysb[cb][:], [], [Ysb[cb]])

    LAG = 1
    load(0)
    if NS > 1:
        load(1)
    for c in range(NCH + LAG):
        if c < NCH:
            pre(c)
        cs = c - LAG
        if cs >= 0:
            seq(cs)
            if cs % SUP == SUP - 1 and cs // SUP + 2 < NS:
                load(cs // SUP + 2)


def phase_a4(K, T, d, out_ap):
    S = K.S
    NT = T // 128

    def rep(name, n=FL):
        t = K.sb([128, n], F32, name)
        K.dma("sync", t[:], d[name].partition_broadcast(128), [t], [])
        return t
    gwr, gbr = rep("gn_w"), rep("gn_b")
    epsb = K.sb([128, 1], F32, "gneps")
    S.op("gpsimd", lambda e: e.memset(epsb[:], GN_EPS), writes=[epsb])
    Yt = [K.sb([128, FL], F32, "Yt") for _ in range(2)]
    Bt = [K.sb([128, FL], F32, "Bt") for _ in range(2)]
    Gt = [K.sb([128, FL], F32, "Gt") for _ in range(2)]
    yc, sq, yn = (K.sb([128, FL], F32, n) for n in ("yc", "sq", "yn"))
    sm = K.sb([128, HL], F32, "sm")
    var = K.sb([128, HL], F32, "var")
    zz = [K.sb([128, FL], BF16, "zz") for _ in range(2)]
    v3 = lambda ap: ap.rearrange("p (h n) -> p h n", h=HL)
    bc = lambda ap: ap.unsqueeze(2).to_broadcast([128, HL, HD])
    for j in range(NT):
        jb, r0 = j % 2, j * 128
        Y, B_, G = Yt[jb], Bt[jb], Gt[jb]
        K.dma("sync", Y[:], d["Y"][r0:r0 + 128, :], [Y], [])
        K.dma("sync", B_[:], d["bonus"][r0:r0 + 128, :], [B_], [])
        K.dma("sync", G[:], d["sgt"][r0:r0 + 128, :], [G], [])
        K.red(sm[:], v3(Y[:]), [sm], [Y])
        K.ts("vector", sm[:], sm[:], 1.0 / HD, ALU.mult, [sm], [sm])
        K.tt("vector", v3(yc[:]), v3(Y[:]), bc(sm[:]), ALU.subtract, [yc], [Y, sm])
        K.tt("gpsimd", sq[:], yc[:], yc[:], ALU.mult, [sq], [yc])
        K.red(var[:], v3(sq[:]), [var], [sq])
        K.act(var[:], var[:], AF.Sqrt, [var], [var, epsb], scale=1.0 / HD, bias=epsb[:])
        S.op("vector", lambda e: e.reciprocal(out=var[:], in_=var[:]), reads=[var], writes=[var])
        K.tt("vector", v3(yn[:]), v3(yc[:]), bc(var[:]), ALU.mult, [yn], [yc, var])
        K.tt("gpsimd", yn[:], yn[:], gwr[:], ALU.mult, [yn], [yn, gwr])
        K.tt("gpsimd", yn[:], yn[:], gbr[:], ALU.add, [yn], [yn, gbr])
        K.tt("gpsimd", yn[:], yn[:], B_[:], ALU.add, [yn], [yn, B_])
        K.tt("vector", zz[jb][:], yn[:], G[:], ALU.mult, [zz[jb]], [yn, G])
        ob = Buf("out")
        K.out_bufs.append(ob)
        K.dma("sync", out_ap[r0:r0 + 128, :], zz[jb][:], [ob], [zz[jb]])


A_IN = (("x", None), ("wA", (D_MODEL, 2176)), ("mu", (128, 8, 6)), ("norm_a", (D_MODEL,)), ("w0", (FL,)), ("a0", (FL,)),
        ("kk", (FL,)), ("ka", (FL,)), ("rk", (FL,)), ("w2", (64, FL)), ("a2", (64, FL)), ("gn_w", (FL,)), ("gn_b", (FL,)))


def declare_A(nc, T, d, internal_kind="Internal"):
    NCH = T // CH
    for n, shp in A_IN:
        shp = (T, D_MODEL) if shp is None else shp
        d[n] = nc.dram_tensor(n, list(shp), F32, kind="ExternalInput").ap()
    for n in ("al", "nbe", "kap", "vb"):
        d[n] = nc.dram_tensor("s_" + n, [T, FL], BF16, kind=internal_kind).ap()
    for n in ("alT", "beT", "rhoT", "kapT"):
        d[n] = nc.dram_tensor("s_" + n, [FL, T], BF16, kind=internal_kind).ap()
    for n in ("bonus", "sgt", "Y"):
        d[n] = nc.dram_tensor("s_" + n, [T, FL], F32, kind=internal_kind).ap()
    d["PC"] = nc.dram_tensor("s_PC", [FL, NCH], F32, kind=internal_kind).ap()


def emit_A(K, T, d, out_ap, phases=(1, 2, 3)):
    cst = None
    if 1 in phases:
        with ExitStack() as pes:
            K.es = pes
            cst = make_consts(K)
            phase_a1(K, T, d, cst)
            K.S.barrier()
    if 2 in phases:
        with ExitStack() as pes:
            K.es = pes
            phase_scan(K, T, d)
            K.S.barrier()
    if 3 in phases:
        with ExitStack() as pes:
            K.es = pes
            phase_a4(K, T, d, out_ap)
            K.S.barrier()


def build_A(T, debug=False, phases=(1, 2, 3)):
    nc = bass.Bass("TRN2", target_bir_lowering=False)
    d = {}
    declare_A(nc, T, d, "ExternalOutput" if debug else "Internal")
    zA = nc.dram_tensor("zA", [T, FL], BF16, kind="ExternalOutput").ap()
    with ExitStack() as es:
        K = Ctx(nc, es)
        emit_A(K, T, d, zA, phases)
        K.S.final_wait("sync", K.out_bufs)
        K.S.emit()
    return nc


def host_inputs_A(inp, b, hh):
    f0, f1 = hh * FL, (hh + 1) * FL
    w = inp["w_in_a"][0]
    cols = np.concatenate([np.arange(0, 1024)[f0:f1], 1088 + np.arange(1024)[f0:f1], 2112 + np.arange(1024)[f0:f1],
                           3200 + np.arange(1024)[f0:f1], np.arange(1024, 1088), np.arange(3136, 3200)])
    mu = inp["mu_a"][0]
    return {
        "x": np.ascontiguousarray(inp["x"][b]),
        "wA": np.ascontiguousarray(w[:, cols]),
        "mu": np.ascontiguousarray(mu.T.reshape(8, 128, 6).transpose(1, 0, 2)),
        "norm_a": np.ascontiguousarray(inp["norm_a"][0]),
        "w0": np.ascontiguousarray(inp["w0_a"][0][f0:f1]), "a0": np.ascontiguousarray(inp["a0_a"][0][f0:f1]),
        "kk": np.ascontiguousarray(inp["kk_a"][0][f0:f1]), "ka": np.ascontiguousarray(inp["ka_a"][0][f0:f1]),
        "rk": np.ascontiguousarray(inp["rk_a"][0].reshape(-1)[f0:f1]),
        "w2": np.ascontiguousarray(inp["w2_a"][0][:, f0:f1]), "a2": np.ascontiguousarray(inp["a2_a"][0][:, f0:f1]),
        "gn_w": np.ascontiguousarray(inp["gn_w_a"][0][f0:f1]), "gn_b": np.ascontiguousarray(inp["gn_b_a"][0][f0:f1]),
    }


NEGB = -30000.0
MBLK = 256
NBLK_MAX = 32


def phase_b0(K, T, d, cst):
    S = K.S
    NT = T // 128
    idb, idf = cst["idb"], cst["idf"]
    WoA = K.sb([128, 8, 1024], BF16, "WoA")
    Wkv = K.sb([128, 8, 1024], BF16, "Wkv")
    Wq = K.sb([128, 8, 1024], BF16, "Wq")
    wst = [K.sb([128, 1024], F32, "wst") for _ in range(2)]
    i = 0
    for (dst, nm) in ((WoA, "woa"), (Wkv, "wkv"), (Wq, "wq")):
        for c in range(8):
            st = wst[i % 2]
            i += 1
            K.dma("sync", st[:], d[nm][c * 128:(c + 1) * 128, :], [st], [])
            K.cp("gpsimd" if i % 2 else "vector", dst[:, c, :], st[:], [dst], [st])

    def rep(name, n):
        t = K.sb([128, n], F32, name)
        K.dma("sync", t[:], d[name].partition_broadcast(128), [t], [])
        return t
    nkvr, nbr = rep("nkv", D_MODEL), rep("nb", D_MODEL)
    knr, qnr = rep("knorm", FL), rep("qnorm", FL)
    K.ts("vector", qnr[:], qnr[:], 0.125, ALU.mult, [qnr], [qnr])
    epsb = K.sb([128, 1], F32, "epsb")
    S.op("gpsimd", lambda e: e.memset(epsb[:], RMS_EPS), writes=[epsb])
    onesc = K.sb([128, 1], F32, "onesc")
    S.op("gpsimd", lambda e: e.memset(onesc[:], 1.0 / MBLK), writes=[onesc])
    kmT = K.sb([128, 4, 2 * NBLK_MAX], F32, "kmT")
    S.op("gpsimd", lambda e: e.memset(kmT[:], 0.0), writes=[kmT])

    Xt = [K.sb([128, D_MODEL], F32, "Xt") for _ in range(2)]
    Zt = [K.sb([128, D_MODEL], BF16, "Zt") for _ in range(2)]
    cs = [K.sb([128, 16], F32, "cs") for _ in range(2)]
    zT = K.sb([128, 8, 128], BF16, "zT")
    x1 = [K.sb([128, D_MODEL], F32, "x1") for _ in range(2)]
    junk = K.sb([128, D_MODEL], BF16, "junk")
    ss, rt, rstd = (K.sb([128, 1], F32, n) for n in ("ss", "rt", "rstd"))
    hkv, hq = K.sb([128, D_MODEL], BF16, "hkv"), K.sb([128, D_MODEL], BF16, "hq")
    hkvT, hqT = K.sb([128, 8, 128], BF16, "hkvT"), K.sb([128, 8, 128], BF16, "hqT")
    f = lambda n: K.sb([128, FL], F32, n)
    sqk, kn, qn = f("sqk"), f("kn"), f("qn")
    ssh = K.sb([128, HL], F32, "ssh")
    rr = K.sb([128, HL], F32, "rr")
    ropet = [K.sb([128, HL, 8], F32, "rope%d" % i) for i in range(4)]
    g2 = lambda n, dt=BF16: [K.sb([128, FL], dt, n) for _ in range(2)]
    knb, qb = g2("knb"), g2("qb")
    vbt = [K.sb([128, HL, 65], BF16, "vbt") for _ in range(2)]
    for t_ in vbt:
        S.op("gpsimd", lambda e, t_=t_: e.memset(t_[:, :, 64:65], 1.0), writes=[t_])
    sgB = g2("sgB", F32)
    kTt = [K.sb([128, 4, 128], BF16, "kTt") for _ in range(2)]
    qTt = [K.sb([128, 4, 128], BF16, "qTt") for _ in range(2)]
    qTf = K.sb([128, 4, 128], F32, "qTf")
    gsm = K.sb([128, HL, NBLK_MAX], F32, "gsm")
    top8 = K.sb([128, HL, 8], F32, "top8")
    MBf = K.sb([128, HL, NBLK_MAX], F32, "MBf")
    MBb = K.sb([128, HL, NBLK_MAX], BF16, "MBb")
    mbT = [K.sb([128, 2, 128], BF16, "mbT") for _ in range(2)]
    p0, p1, pK, pV, pQ, pG, pkm = [K.ps([128, 512], F32, n) for n in ("p0", "p1", "pK", "pV", "pQ", "pG", "pkm")]
    pT = K.ps([128, 1024], BF16, "pT")
    v3 = lambda ap: ap.rearrange("p (h n) -> p h n", h=HL)
    bc = lambda ap: ap.unsqueeze(2).to_broadcast([128, HL, HD])

    def norm_rope(src_ps, dst, nrep, CS):
        K.act(sqk[:], src_ps[:, :], AF.Square, [sqk], [src_ps])
        K.red(ssh[:], v3(sqk[:]), [ssh], [sqk])
        K.act(rr[:], ssh[:], AF.Sqrt, [rr], [ssh, epsb], scale=1.0 / HD, bias=epsb[:])
        S.op("vector", lambda e: e.reciprocal(out=rr[:], in_=rr[:]), reads=[rr], writes=[rr])
        K.tt("vector", v3(dst[:]), v3(src_ps[:, :]), bc(rr[:]), ALU.mult, [dst], [src_ps, rr])
        K.tt("gpsimd", dst[:], dst[:], nrep[:], ALU.mult, [dst], [dst, nrep])
        d3 = v3(dst[:])
        xa, xb_ = d3[:, :, 0:8], d3[:, :, 8:16]
        cosb = CS[:, 0:8].unsqueeze(1).to_broadcast([128, HL, 8])
        sinb = CS[:, 8:16].unsqueeze(1).to_broadcast([128, HL, 8])
        ta, tb, tc, td = ropet
        K.tt("gpsimd", ta[:], xa, cosb, ALU.mult, [ta], [dst, CS])
        K.tt("gpsimd", tb[:], xb_, sinb, ALU.mult, [tb], [dst, CS])
        K.tt("gpsimd", tc[:], xb_, cosb, ALU.mult, [tc], [dst, CS])
        K.tt("gpsimd", td[:], xa, sinb, ALU.mult, [td], [dst, CS])
        K.tt("gpsimd", xa, ta[:], tb[:], ALU.subtract, [dst], [ta, tb])
        K.tt("gpsimd", xb_, tc[:], td[:], ALU.add, [dst], [tc, td])

    for j in range(NT):
        jb, r0, cb = j % 2, j * 128, (j * 128) // MBLK
        X, Z, CS, X1 = Xt[jb], Zt[jb], cs[jb], x1[jb]
        K.dma("sync", X[:], d["x"][r0:r0 + 128, :], [X], [])
        K.dma("sync", Z[:], d["zA"][r0:r0 + 128, :], [Z], [])
        K.dma("sync", CS[:], d["cs"][r0:r0 + 128, :], [CS], [])
        for c in range(8):
            K.tr(pT[:, c * 128:(c + 1) * 128], Z[:, c * 128:(c + 1) * 128], idb[:], [pT], [Z, idb], inc=(c == 7))
        K.cp("scalar", zT[:], pT[:].rearrange("p (c t) -> p c t", c=8), [zT], [pT])
        for half, pp in enumerate((p0, p1)):
            for c in range(8):
                K.mm(pp[:, :], zT[:, c, :], WoA[:, c, half * 512:(half + 1) * 512], [pp], [zT, WoA], start=(c == 0), stop=(c == 7), inc=(c == 7))
            K.tt("vector", X1[:, half * 512:(half + 1) * 512], pp[:, :], X[:, half * 512:(half + 1) * 512], ALU.add, [X1], [pp, X])
        ob = Buf("x1o")
        K.out_bufs.append(ob)
        K.dma("sync", d["x1"][r0:r0 + 128, :], X1[:], [ob], [X1])
        K.act(junk[:], X1[:], AF.Square, [junk, ss], [X1], accum=ss[:])
        K.act(rt[:], ss[:], AF.Sqrt, [rt], [ss, epsb], scale=1.0 / D_MODEL, bias=epsb[:])
        S.op("vector", lambda e: e.reciprocal(out=rstd[:], in_=rt[:]), reads=[rt], writes=[rstd])
        K.stt(hkv[:], X1[:], rstd[:, 0:1], nkvr[:], ALU.mult, ALU.mult, [hkv], [X1, rstd, nkvr])
        K.stt(hq[:], X1[:], rstd[:, 0:1], nbr[:], ALU.mult, ALU.mult, [hq], [X1, rstd, nbr])
        for (src, dstT) in ((hkv, hkvT), (hq, hqT)):
            for c in range(8):
                K.tr(pT[:, c * 128:(c + 1) * 128], src[:, c * 128:(c + 1) * 128], idb[:], [pT], [src, idb], inc=(c == 7))
            K.cp("scalar", dstT[:], pT[:].rearrange("p (c t) -> p c t", c=8), [dstT], [pT])
        for (pp, hT_, Wt, o) in ((pK, hkvT, Wkv, 0), (pV, hkvT, Wkv, 512), (pQ, hqT, Wq, 0), (pG, hqT, Wq, 512)):
            for c in range(8):
                K.mm(pp[:, :], hT_[:, c, :], Wt[:, c, o:o + 512], [pp], [hT_, Wt], start=(c == 0), stop=(c == 7), inc=(c == 7))
        K.cp("scalar", vbt[jb][:, :, 0:64], v3(pV[:, :]), [vbt[jb]], [pV])
        K.act(sgB[jb][:], pG[:, :], AF.Silu, [sgB[jb]], [pG])
        K.dma("sync", d["V"][r0:r0 + 128, :], vbt[jb][:].rearrange("p h c -> p (h c)"), [], [vbt[jb]])
        K.dma("sync", d["sgB"][r0:r0 + 128, :], sgB[jb][:], [], [sgB[jb]])
        norm_rope(pK, kn, knr, CS)
        K.cp("scalar", knb[jb][:], kn[:], [knb[jb]], [kn])
        for pr_ in range(4):
            col = pr_ * 2 + (j % 2)
            K.mm(pkm[:, col:col + 1], kn[:, pr_ * 128:(pr_ + 1) * 128], onesc[:], [pkm], [kn, onesc], inc=(pr_ == 3))
        norm_rope(pQ, qn, qnr, CS)
        K.cp("scalar", qb[jb][:], qn[:], [qb[jb]], [qn])
        MB3 = MBf
        S.op("gpsimd", lambda e: e.memset(MB3[:], NEGB), writes=[MB3])
        S.op("gpsimd", lambda e, cb=cb: e.memset(MB3[:, :, cb:cb + 1], 0.0), reads=[MB3], writes=[MB3])
        if cb > 0:
            for fc in range(4):
                K.tr(p0[:, fc * 128:(fc + 1) * 128], qn[:, fc * 128:(fc + 1) * 128], idf[:], [p0], [qn, idf], inc=(fc == 3))
            K.cp("vector", qTf[:], p0[:, :].rearrange("p (c t) -> p c t", c=4), [qTf], [p0])
            for pr_ in range(4):
                K.mm(p1[:, pr_ * 64:(pr_ + 1) * 64], qTf[:, pr_, :], kmT[:, pr_, :], [p1], [qTf, kmT], inc=(pr_ == 3))
            S.op("gpsimd", lambda e: e.memset(gsm[:], -1e30), writes=[gsm])
            K.cp("vector", gsm[:, :, 0:cb], p1[:, 0:256].rearrange("p (h n) -> p h n", h=HL)[:, :, 0:cb], [gsm], [p1])
            for h in range(HL):
                S.op("vector", lambda e, h=h: e.max(out=top8[:, h, :], in_=gsm[:, h, :]), reads=[gsm], writes=[top8])
            K.tt("vector", MB3[:, :, 0:cb], gsm[:, :, 0:cb], top8[:, :, 2:3].to_broadcast([128, HL, cb]), ALU.is_ge, [MB3], [gsm, top8])
            K.ts("vector", MB3[:, :, 0:cb], MB3[:, :, 0:cb], -1.0, ALU.add, [MB3], [MB3], -NEGB, ALU.mult)
        K.cp("gpsimd", MBb[:], MB3[:], [MBb], [MB3])
        if j % 2 == 1:
            for (lo, hi, co) in ((0, 64, cb), (64, 128, NBLK_MAX + cb)):
                pk3 = pkm[lo:hi, 0:8].rearrange("p (a b) -> p a b", b=2)
                K.cp("vector", kmT[lo:hi, :, co:co + 1], pk3[:, :, 0:1], [kmT], [pkm])
                K.tt("vector", kmT[lo:hi, :, co:co + 1], kmT[lo:hi, :, co:co + 1], pk3[:, :, 1:2], ALU.add, [kmT], [kmT, pkm])
        for qi, q in enumerate((knb[jb], qb[jb])):
            for fc in range(4):
                K.tr(pT[:, qi * 512 + fc * 128:qi * 512 + (fc + 1) * 128], q[:, fc * 128:(fc + 1) * 128], idb[:], [pT], [q, idb],
                     inc=(qi == 1 and fc == 3))
        K.cp("scalar", kTt[jb][:], pT[:, 0:512].rearrange("p (c t) -> p c t", c=4), [kTt[jb]], [pT])
        K.cp("vector", qTt[jb][:], pT[:, 512:1024].rearrange("p (c t) -> p c t", c=4), [qTt[jb]], [pT])
        for hg in range(2):
            K.tr(pT[:, hg * 128:(hg + 1) * 128], MBb[:, hg * 4:(hg + 1) * 4, :].rearrange("p h n -> p (h n)"), idb[:], [pT], [MBb, idb], inc=(hg == 1))
        K.cp("scalar", mbT[jb][:], pT[:, 0:256].rearrange("p (g t) -> p g t", g=2), [mbT[jb]], [pT])
        K.dma("gpsimd", d["KT"].rearrange("(fc p) t -> p fc t", p=128)[:, :, r0:r0 + 128], kTt[jb][:], [], [kTt[jb]])
        K.dma("gpsimd", d["QT"].rearrange("(fc p) t -> p fc t", p=128)[:, :, r0:r0 + 128], qTt[jb][:], [], [qTt[jb]])
        K.dma("gpsimd", d["MBT"].rearrange("(hg hl) n t -> (hl n) hg t", hg=2)[:, :, r0:r0 + 128], mbT[jb][:], [], [mbT[jb]])


def phase_b1(K, T, d, cst):
    S = K.S
    idb, idf = cst["idb"], cst["idf"]
    NG = T // 512
    NKT = T // 128
    NB = T // MBLK
    KTa = [K.sb([96, T], BF16, "KTa") for _ in range(2)]
    QTa = [K.sb([96, T], BF16, "QTa") for _ in range(2)]
    Va = K.sb([128, NKT, HL * 65], BF16, "Va")
    nvd = min(8, NKT)
    for i in range(nvd):
        k0, k1 = i * NKT // nvd, (i + 1) * NKT // nvd
        K.dma("sync", Va[:, k0:k1, :], d["V"][k0 * 128:k1 * 128, :].rearrange("(kt p) f -> p kt f", p=128), [Va], [])
    for t_ in KTa:
        S.op("gpsimd", lambda e, t_=t_: e.memset(t_[64:96, :], 1.0), writes=[t_])
        S.op("gpsimd", lambda e, t_=t_: e.affine_select(out=t_[64:96, :], in_=t_[64:96, :], pattern=[[1, T]], compare_op=ALU.is_ge,
                                                        fill=0.0, base=0, channel_multiplier=-MBLK), reads=[t_], writes=[t_])
        S.op("gpsimd", lambda e, t_=t_: e.affine_select(out=t_[64:96, :], in_=t_[64:96, :], pattern=[[-1, T]], compare_op=ALU.is_ge,
                                                        fill=0.0, base=MBLK - 1, channel_multiplier=MBLK), reads=[t_], writes=[t_])
    CB = []
    for i in range(4):
        t_ = K.sb([128, 512], BF16, "CB%d" % i)
        S.op("gpsimd", lambda e, t_=t_: e.memset(t_[:], 0.0), writes=[t_])
        bq0 = (i // 2) * 256
        ki = i % 2
        if ki == 1:
            S.op("gpsimd", lambda e, t_=t_, bq0=bq0: e.memset(t_[:, bq0:bq0 + 128], NEGB), reads=[t_], writes=[t_])
        dq = bq0 + ki * 128
        S.op("gpsimd", lambda e, t_=t_, dq=dq: e.affine_select(out=t_[:, dq:dq + 128], in_=t_[:, dq:dq + 128], pattern=[[1, 128]],
                                                               compare_op=ALU.is_ge, fill=NEGB, base=0, channel_multiplier=-1),
             reads=[t_], writes=[t_])
        CB.append(t_)
    PT = [K.sb([128, 512], BF16, "PT") for _ in range(3)]
    Osb = [K.sb([65, 512], F32, "Osb") for _ in range(2)]
    rec = K.sb([128, 4], F32, "rec")
    Oo = [K.sb([128, 4, 64], F32, "Oo") for _ in range(2)]
    pS = [K.ps([128, 512], F32, "pS") for _ in range(3)]
    pO = [K.ps([128, 512], F32, "pO") for _ in range(2)]
    pTr = K.ps([128, 512], F32, "pTr")
    cnt = 0
    for h in range(HL):
        hb_ = h % 2
        KT_, QT_, V_ = KTa[hb_], QTa[hb_], Va
        K.dma("sync", KT_[0:64, :], d["KT"][h * 64:(h + 1) * 64, :], [KT_], [])
        K.dma("sync", QT_[0:64, :], d["QT"][h * 64:(h + 1) * 64, :], [QT_], [])
        K.dma("sync", QT_[64:96, :], d["MBT"][h, :, :], [QT_], [])
        for g in range(NG):
            qs = slice(g * 512, (g + 1) * 512)
            po = pO[g % 2]
            nkt = 4 * g + 4
            for kt in range(nkt):
                ps_ = pS[cnt % 3]
                pt_ = PT[cnt % 3]
                cnt += 1
                diag = kt >= 4 * g
                K.mm(ps_[:, :], KT_[:, kt * 128:(kt + 1) * 128], QT_[:, qs], [ps_], [KT_, QT_], start=True, stop=not diag, inc=not diag)
                if diag:
                    K.mm(ps_[:, :], idb[:], CB[kt - 4 * g][:], [ps_], [idb, CB[kt - 4 * g]], start=False, stop=True)
                K.act(pt_[:], ps_[:, :], AF.Exp, [pt_], [ps_])
                K.mm(po[0:65, :], V_[:, kt, h * 65:(h + 1) * 65], pt_[:], [po], [V_, pt_], start=(kt == 0), stop=(kt == nkt - 1), inc=(kt == nkt - 1))
            ob_ = Osb[g % 2]
            K.cp("vector", ob_[:], po[0:65, :], [ob_], [po])
            for jq in range(4):
                K.tr(pTr[:, jq * 65:(jq + 1) * 65], ob_[:, jq * 128:(jq + 1) * 128], idf[0:65, 0:65], [pTr], [ob_, idf], inc=(jq == 3))
            p3 = pTr[:, 0:260].rearrange("p (j c) -> p j c", c=65)
            S.op("vector", lambda e, p3=p3: e.reciprocal(out=rec[:].unsqueeze(2), in_=p3[:, :, 64:65]), reads=[pTr], writes=[rec])
            oo = Oo[g % 2]
            K.tt("vector", oo[:], p3[:, :, 0:64], rec[:].unsqueeze(2).to_broadcast([128, 4, 64]), ALU.mult, [oo], [pTr, rec])
            K.dma("sync", d["O"][g * 512:(g + 1) * 512, h * 64:(h + 1) * 64].rearrange("(j p) f -> p j f", p=128), oo[:], [], [oo])


def phase_b2(K, T, d, out_ap):
    NT = T // 128
    Ot = [K.sb([128, FL], F32, "Ot") for _ in range(2)]
    Gt = [K.sb([128, FL], F32, "Gt") for _ in range(2)]
    zz = [K.sb([128, FL], BF16, "zz") for _ in range(2)]
    for j in range(NT):
        jb, r0 = j % 2, j * 128
        K.dma("sync", Ot[jb][:], d["O"][r0:r0 + 128, :], [Ot[jb]], [])
        K.dma("sync", Gt[jb][:], d["sgB"][r0:r0 + 128, :], [Gt[jb]], [])
        K.tt("vector", zz[jb][:], Ot[jb][:], Gt[jb][:], ALU.mult, [zz[jb]], [Ot[jb], Gt[jb]])
        ob = Buf("zBo")
        K.out_bufs.append(ob)
        K.dma("sync", out_ap[r0:r0 + 128, :], zz[jb][:], [ob], [zz[jb]])


B_IN = (("x", None, F32), ("zA", None, BF16), ("woa", (D_MODEL, D_MODEL), F32), ("wkv", (D_MODEL, 1024), F32), ("wq", (D_MODEL, 1024), F32),
        ("nkv", (D_MODEL,), F32), ("nb", (D_MODEL,), F32), ("knorm", (FL,), F32), ("qnorm", (FL,), F32), ("cs", "cs", F32))


def declare_B(nc, T, d, internal_kind="Internal"):
    for n, shp, dt in B_IN:
        shp = (T, D_MODEL) if shp is None else ((T, 16) if shp == "cs" else shp)
        d[n] = nc.dram_tensor(n, list(shp), dt, kind="ExternalInput").ap()
    d["KT"] = nc.dram_tensor("s_KT", [FL, T], BF16, kind=internal_kind).ap()
    d["QT"] = nc.dram_tensor("s_QT", [FL, T], BF16, kind=internal_kind).ap()
    d["MBT"] = nc.dram_tensor("s_MBT", [HL, NBLK_MAX, T], BF16, kind=internal_kind).ap()
    d["V"] = nc.dram_tensor("s_V", [T, HL * 65], BF16, kind=internal_kind).ap()
    d["sgB"] = nc.dram_tensor("s_sgB", [T, FL], F32, kind=internal_kind).ap()
    d["O"] = nc.dram_tensor("s_O", [T, FL], F32, kind=internal_kind).ap()


def emit_B(K, T, d, out_ap, phases=(1, 2, 3)):
    if 1 in phases:
        with ExitStack() as pes:
            K.es = pes
            cst = make_consts(K)
            phase_b0(K, T, d, cst)
            K.S.barrier()
    if 2 in phases:
        with ExitStack() as pes:
            K.es = pes
            cst = make_consts(K)
            phase_b1(K, T, d, cst)
            K.S.barrier()
    if 3 in phases:
        with ExitStack() as pes:
            K.es = pes
            phase_b2(K, T, d, out_ap)
            K.S.barrier()


def build_B(T, debug=False, phases=(1, 2, 3)):
    nc = bass.Bass("TRN2", target_bir_lowering=False)
    d = {}
    declare_B(nc, T, d, "ExternalOutput" if debug else "Internal")
    d["x1"] = nc.dram_tensor("x1", [T, D_MODEL], F32, kind="ExternalOutput").ap()
    zB = nc.dram_tensor("zB", [T, FL], BF16, kind="ExternalOutput").ap()
    with ExitStack() as es:
        K = Ctx(nc, es)
        emit_B(K, T, d, zB, phases)
        K.S.final_wait("sync", K.out_bufs)
        K.S.emit()
    return nc


def rope_cs(T):
    inv = np.power(np.float32(500000.0), -np.arange(0, 16, 2, dtype=np.float32) / np.float32(16)).astype(np.float32)
    ang = (np.arange(T, dtype=np.float32)[:, None] * inv[None, :]).astype(np.float32)
    return np.concatenate([np.cos(ang), np.sin(ang)], axis=1).astype(np.float32)


def host_inputs_B(inp, b, hh, zA_full, T):
    f0, f1 = hh * FL, (hh + 1) * FL
    wkv = inp["w_kv"]
    wq = inp["w_in_b"][0]
    return {
        "x": np.ascontiguousarray(inp["x"][b][:T]),
        "zA": np.ascontiguousarray(zA_full),
        "woa": np.ascontiguousarray(inp["w_out_a"][0]),
        "wkv": np.ascontiguousarray(np.concatenate([wkv[:, f0:f1], wkv[:, 1024 + f0:1024 + f1]], axis=1)),
        "wq": np.ascontiguousarray(np.concatenate([wq[:, f0:f1], wq[:, 1024 + f0:1024 + f1]], axis=1)),
        "nkv": np.ascontiguousarray(inp["norm_kv"]), "nb": np.ascontiguousarray(inp["norm_b"][0]),
        "knorm": np.ascontiguousarray(np.tile(inp["k_norm"], HL)), "qnorm": np.ascontiguousarray(np.tile(inp["q_norm_b"][0], HL)),
        "cs": rope_cs(T),
    }


def build_C(R):
    nc = bass.Bass("TRN2", target_bir_lowering=False)
    x1 = nc.dram_tensor("x1", [R, D_MODEL], F32, kind="ExternalInput").ap()
    zB = nc.dram_tensor("zB", [R, D_MODEL], BF16, kind="ExternalInput").ap()
    wob = nc.dram_tensor("wob", [D_MODEL, D_MODEL], F32, kind="ExternalInput").ap()
    out = nc.dram_tensor("out", [R, D_MODEL], F32, kind="ExternalOutput").ap()
    with ExitStack() as es:
        K = Ctx(nc, es)
        cst = make_consts(K)
        idb = cst["idb"]
        Wo = K.sb([128, 8, 1024], BF16, "Wo")
        wst = [K.sb([128, 1024], F32, "wst") for _ in range(2)]
        for c in range(8):
            K.dma("sync", wst[c % 2][:], wob[c * 128:(c + 1) * 128, :], [wst[c % 2]], [])
            K.cp("gpsimd" if c % 2 else "vector", Wo[:, c, :], wst[c % 2][:], [Wo], [wst[c % 2]])
        Xt = [K.sb([128, D_MODEL], F32, "Xt") for _ in range(2)]
        Zt = [K.sb([128, D_MODEL], BF16, "Zt") for _ in range(2)]
        zT = [K.sb([128, 8, 128], BF16, "zT") for _ in range(2)]
        Ot = [K.sb([128, D_MODEL], F32, "Ot") for _ in range(2)]
        pT = [K.ps([128, 1024], BF16, "pT") for _ in range(2)]
        pp = [K.ps([128, 512], F32, "pp") for _ in range(4)]
        for j in range(R // 128):
            jb, r0 = j % 2, j * 128
            K.dma("sync", Xt[jb][:], x1[r0:r0 + 128, :], [Xt[jb]], [])
            K.dma("sync", Zt[jb][:], zB[r0:r0 + 128, :], [Zt[jb]], [])
            for c in range(8):
                K.tr(pT[jb][:, c * 128:(c + 1) * 128], Zt[jb][:, c * 128:(c + 1) * 128], idb[:], [pT[jb]], [Zt[jb], idb], inc=(c == 7))
            K.cp("scalar", zT[jb][:], pT[jb][:].rearrange("p (c t) -> p c t", c=8), [zT[jb]], [pT[jb]])
            for half in range(2):
                p_ = pp[jb * 2 + half]
                for c in range(8):
                    K.mm(p_[:, :], zT[jb][:, c, :], Wo[:, c, half * 512:(half + 1) * 512], [p_], [zT[jb], Wo], start=(c == 0), stop=(c == 7), inc=(c == 7))
                K.tt("vector", Ot[jb][:, half * 512:(half + 1) * 512], p_[:, :], Xt[jb][:, half * 512:(half + 1) * 512], ALU.add, [Ot[jb]], [p_, Xt[jb]])
            ob = Buf("o")
            K.out_bufs.append(ob)
            K.dma("sync", out[r0:r0 + 128, :], Ot[jb][:], [ob], [Ot[jb]])
        K.S.final_wait("sync", K.out_bufs)
        K.S.emit()
    return nc


_CACHE = {}


def kernel(**inp):
    inp = {k: np.asarray(v) for k, v in inp.items()}
    Bn, T, D = inp["x"].shape
    cores = list(range(8))
    if "A" not in _CACHE:
        _CACHE["A"] = build_A(T)
    insA = [host_inputs_A(inp, c // 2, c % 2) for c in cores]
    resA = run_bass_kernel_spmd(_CACHE["A"], insA, core_ids=cores).results
    zA = [np.concatenate([np.asarray(resA[2 * b]["zA"]), np.asarray(resA[2 * b + 1]["zA"])], axis=1) for b in range(Bn)]
    if "B" not in _CACHE:
        _CACHE["B"] = build_B(T)
    insB = [host_inputs_B(inp, c // 2, c % 2, zA[c // 2], T) for c in cores]
    resB = run_bass_kernel_spmd(_CACHE["B"], insB, core_ids=cores).results
    x1 = np.concatenate([np.asarray(resB[2 * b]["x1"]) for b in range(Bn)], axis=0)
    zB = np.concatenate([np.concatenate([np.asarray(resB[2 * b]["zB"]), np.asarray(resB[2 * b + 1]["zB"])], axis=1) for b in range(Bn)], axis=0)
    R = Bn * T // 8
    if "C" not in _CACHE:
        _CACHE["C"] = build_C(R)
    wob = np.ascontiguousarray(inp["w_out_b"][0])
    insC = [{"x1": np.ascontiguousarray(x1[c * R:(c + 1) * R]), "zB": np.ascontiguousarray(zB[c * R:(c + 1) * R]), "wob": wob} for c in cores]
    resC = run_bass_kernel_spmd(_CACHE["C"], insC, core_ids=cores).results
    out = np.concatenate([np.asarray(resC[c]["out"]) for c in cores], axis=0).reshape(Bn, T, D)
    return out.astype(np.float32)
```
